# Optimizing a Trainium2 kernel written in Bass

```python
import math
import jax, jax.numpy as jnp
from jax import lax
import numpy as np

D_MODEL = 1024
BATCH = 8
SEQ = 2048
DEPTH = 4

D_MIX = D_MODEL
D_ATTN = D_MIX // 2
D_LRU = D_MIX - D_ATTN
HEAD_DIM = 64
V_DIM = 2 * HEAD_DIM
N_ATTN_HEADS = D_ATTN // V_DIM
N_LRU_BLOCKS = 8
LRU_BLOCK = D_LRU // N_LRU_BLOCKS
CONV_WIDTH = 4
CONV_PAD = (CONV_WIDTH // 2, CONV_WIDTH - 1 - CONV_WIDTH // 2)
LRU_C = 8.0
MIX_IN = 3 * D_ATTN + 2 * D_LRU
D_FF = ((8 * D_MODEL // 3 + 127) // 128) * 128
ROPE_THETA = 10000.0
Q_BLOCK = 128
EPS = 1e-6
N_SUB = 3

kernel_name = "hybrid_diffattn_rglru_macaron_encoder"


def rms_norm(x, g):
    xf = x.astype(jnp.float32)
    y = xf * lax.rsqrt(jnp.mean(xf * xf, axis=-1, keepdims=True) + EPS)
    return (y * g.astype(jnp.float32)).astype(x.dtype)


def rope_tables(positions):
    inv = ROPE_THETA ** (-jnp.arange(0, HEAD_DIM, 2, dtype=jnp.float32) / HEAD_DIM)
    ang = positions.astype(jnp.float32)[..., None] * inv
    return jnp.cos(ang), jnp.sin(ang)


def apply_rope(t, cos, sin):
    cos = cos[:, :, None, None, :]
    sin = sin[:, :, None, None, :]
    t1, t2 = jnp.split(t.astype(jnp.float32), 2, axis=-1)
    out = jnp.concatenate([t1 * cos - t2 * sin, t2 * cos + t1 * sin], axis=-1)
    return out.astype(t.dtype)


def diff_attention(q, k, v, lam, subln_g, lambda_init, cos, sin):
    B, S = q.shape[0], q.shape[1]
    q = apply_rope(q, cos, sin) * (HEAD_DIM ** -0.5)
    k = apply_rope(k, cos, sin)
    nb = S // Q_BLOCK
    qb = q.reshape(B, nb, Q_BLOCK, N_ATTN_HEADS, 2, HEAD_DIM).transpose(1, 0, 2, 3, 4, 5)

    def block(qblk):
        s = jnp.einsum('bqhmd,bkhmd->bhmqk', qblk, k, preferred_element_type=jnp.float32)
        p = jax.nn.softmax(s, axis=-1)
        w = p[:, :, 0] - lam * p[:, :, 1]
        return jnp.einsum('bhqk,bkhe->bqhe', w.astype(v.dtype), v)

    o = lax.map(block, qb)
    o = o.transpose(1, 0, 2, 3, 4).reshape(B, S, N_ATTN_HEADS, V_DIM)
    o = rms_norm(o, subln_g) * (1.0 - lambda_init)
    return o.reshape(B, S, D_ATTN)


def _lin_combine(e1, e2):
    a1, b1 = e1
    a2, b2 = e2
    return (a1 * a2, a2 * b1 + b2)


def rg_lru(xc, w_gate, b_gate, lam, reverse):
    B, S = xc.shape[0], xc.shape[1]
    xb = xc.reshape(B, S, N_LRU_BLOCKS, LRU_BLOCK)
    g = jnp.einsum('bsni,gnij->gbsnj', xb, w_gate).reshape(2, B, S, D_LRU)
    g = jax.nn.sigmoid(g.astype(jnp.float32) + b_gate.astype(jnp.float32)[:, None, None, :])
    r, i = g[0], g[1]
    log_a = -LRU_C * r * jax.nn.softplus(-lam.astype(jnp.float32))
    a = jnp.exp(log_a)
    mult = jnp.sqrt(-jnp.expm1(2.0 * log_a))
    b = mult * i * xc.astype(jnp.float32)
    _, h = lax.associative_scan(_lin_combine, (a, b), axis=1, reverse=reverse)
    return h


def recurrent_group(xr, yr, conv_w, conv_b, w_gate, b_gate, lam):
    xc = lax.conv_general_dilated(
        xr, conv_w[:, None, :].astype(xr.dtype), window_strides=(1,), padding=[CONV_PAD],
        dimension_numbers=('NWC', 'WIO', 'NWC'), feature_group_count=D_LRU) + conv_b
    h = rg_lru(xc, w_gate[0], b_gate[0], lam[0], False) + rg_lru(xc, w_gate[1], b_gate[1], lam[1], True)
    return (h * jax.nn.gelu(yr.astype(jnp.float32))).astype(xr.dtype)


def swiglu(h, w_in, w_out):
    gu = h @ w_in
    g, u = jnp.split(gu, 2, axis=-1)
    return (jax.nn.silu(g) * u) @ w_out


def setup_inputs(seed: int = 0) -> dict:
    key = jax.random.key(seed)
    ks = jax.random.split(key, 20)
    f32 = jnp.float32
    nrm = lambda k, shape, s: jax.random.normal(k, shape, f32) * s
    u = jax.random.uniform(ks[18], (DEPTH, 2, D_LRU), f32, 0.9, 0.999)
    s = u ** (1.0 / LRU_C)
    lru_lambda = jnp.log(s) - jnp.log1p(-s)
    return {
        "x": nrm(ks[0], (BATCH, SEQ, D_MODEL), 1.0),
        "c": nrm(ks[1], (BATCH, D_MODEL), 1.0),
        "positions": jnp.broadcast_to(jnp.arange(SEQ, dtype=jnp.int32), (BATCH, SEQ)),
        "w_ada": nrm(ks[2], (DEPTH, D_MODEL, N_SUB * 3 * D_MODEL), 0.5 * D_MODEL ** -0.5),
        "b_ada": nrm(ks[3], (DEPTH, N_SUB * 3 * D_MODEL), 0.02),
        "norm_g": 1.0 + nrm(ks[4], (DEPTH, 2 * N_SUB, D_MODEL), 0.05),
        "ffn1_w_in": nrm(ks[5], (DEPTH, D_MODEL, 2 * D_FF), D_MODEL ** -0.5),
        "ffn1_w_out": nrm(ks[6], (DEPTH, D_FF, D_MODEL), D_FF ** -0.5),
        "ffn2_w_in": nrm(ks[7], (DEPTH, D_MODEL, 2 * D_FF), D_MODEL ** -0.5),
        "ffn2_w_out": nrm(ks[8], (DEPTH, D_FF, D_MODEL), D_FF ** -0.5),
        "w_mix_in": nrm(ks[9], (DEPTH, D_MODEL, MIX_IN), D_MODEL ** -0.5),
        "w_mix_out": nrm(ks[10], (DEPTH, D_MIX, D_MODEL), D_MIX ** -0.5),
        "lambda_q": nrm(ks[11], (DEPTH, 2, HEAD_DIM), 0.1),
        "lambda_k": nrm(ks[12], (DEPTH, 2, HEAD_DIM), 0.1),
        "subln_g": 1.0 + nrm(ks[13], (DEPTH, V_DIM), 0.05),
        "conv_w": nrm(ks[14], (DEPTH, CONV_WIDTH, D_LRU), CONV_WIDTH ** -0.5),
        "conv_b": nrm(ks[15], (DEPTH, D_LRU), 0.02),
        "lru_w_gate": nrm(ks[16], (DEPTH, 2, 2, N_LRU_BLOCKS, LRU_BLOCK, LRU_BLOCK), LRU_BLOCK ** -0.5),
        "lru_b_gate": nrm(ks[17], (DEPTH, 2, 2, D_LRU), 0.1),
        "lru_lambda": lru_lambda,
    }


def reference(x, c, positions, w_ada, b_ada, norm_g, ffn1_w_in, ffn1_w_out, ffn2_w_in, ffn2_w_out,
              w_mix_in, w_mix_out, lambda_q, lambda_k, subln_g, conv_w, conv_b,
              lru_w_gate, lru_b_gate, lru_lambda):
    B, S, D = x.shape
    cos, sin = rope_tables(positions)
    cond = jax.nn.silu(c)
    split_pts = [D_ATTN, 2 * D_ATTN, 3 * D_ATTN, 3 * D_ATTN + D_LRU]

    for l in range(DEPTH):
        ada = (cond @ w_ada[l] + b_ada[l]).reshape(B, N_SUB, 3, D)

        def sublayer(h_res, j, fn, res_w):
            shift = ada[:, j, 0][:, None, :]
            scale = ada[:, j, 1][:, None, :]
            gate = ada[:, j, 2][:, None, :]
            h = rms_norm(h_res, norm_g[l, 2 * j]) * (1.0 + scale) + shift
            y = rms_norm(fn(h), norm_g[l, 2 * j + 1])
            return h_res + res_w * gate * y

        lambda_init = 0.8 - 0.6 * math.exp(-0.3 * l)
        lam = (jnp.exp(jnp.sum(lambda_q[l, 0].astype(jnp.float32) * lambda_k[l, 0].astype(jnp.float32)))
               - jnp.exp(jnp.sum(lambda_q[l, 1].astype(jnp.float32) * lambda_k[l, 1].astype(jnp.float32)))
               + lambda_init)

        def mixer(h):
            z = h @ w_mix_in[l]
            q, k, v, xr, yr = jnp.split(z, split_pts, axis=-1)
            q = q.reshape(B, S, N_ATTN_HEADS, 2, HEAD_DIM)
            k = k.reshape(B, S, N_ATTN_HEADS, 2, HEAD_DIM)
            v = v.reshape(B, S, N_ATTN_HEADS, V_DIM)
            attn = diff_attention(q, k, v, lam, subln_g[l], lambda_init, cos, sin)
            rec = recurrent_group(xr, yr, conv_w[l], conv_b[l], lru_w_gate[l], lru_b_gate[l], lru_lambda[l])
            return jnp.concatenate([attn, rec], axis=-1) @ w_mix_out[l]

        x = sublayer(x, 0, lambda h: swiglu(h, ffn1_w_in[l], ffn1_w_out[l]), 0.5)
        x = sublayer(x, 1, mixer, 1.0)
        x = sublayer(x, 2, lambda h: swiglu(h, ffn2_w_in[l], ffn2_w_out[l]), 0.5)
    return x
```

```python
import os
import math
import numpy as np
from contextlib import ExitStack
import concourse.bass as bass
import concourse.mybir as mybir
from concourse.bass_utils import run_bass_kernel_spmd

F32 = mybir.dt.float32
BF16 = mybir.dt.bfloat16
I32 = mybir.dt.int32
AF = mybir.ActivationFunctionType
ALU = mybir.AluOpType
AX = mybir.AxisListType

P = 128
S = 2048
D = 1024
NT = 16
KC = 8
DFF = 2816
FC = 22
DEPTH = 4
EPS = 1e-6
NSLOT = 8
ARW = 25088
MIXC = 3584


class Buf:
    __slots__ = ("name", "lw", "rd", "arena")

    def __init__(self, name, arena=False):
        self.name = name
        self.lw = None
        self.rd = []
        self.arena = arena


class Op:
    __slots__ = ("eng", "fn", "deps", "idx", "is_dma", "dma_slot", "dma_val", "need_inc", "inc_val", "nq")


class Sched:
    DMA_SLOTS = {"sp": (0, 10), "pool": (10, 12), "act": (22, 2)}

    def __init__(self):
        self.ops = []
        self.frontier = []
        self.arena_ops = {}
        self.arena_dmas = []

    def op(self, eng, fn, reads=(), writes=(), dma=False, nq=1):
        o = Op()
        o.eng, o.fn, o.is_dma, o.nq = eng, fn, dma, nq
        o.deps = set()
        o.need_inc = False
        o.inc_val = None
        o.dma_slot = None
        o.dma_val = None
        o.idx = len(self.ops)
        touches = False
        for b in reads:
            if b.lw is not None:
                o.deps.add(b.lw)
            touches = touches or b.arena
        for b in writes:
            if b.lw is not None:
                o.deps.add(b.lw)
            for r in b.rd:
                o.deps.add(r)
            touches = touches or b.arena
        for b in reads:
            b.rd.append(o)
        for b in writes:
            b.lw = o
            b.rd = []
        if touches:
            for f in self.frontier:
                o.deps.add(f)
            if dma:
                self.arena_dmas.append(o)
            else:
                self.arena_ops[eng] = o
        o.deps.discard(o)
        best = {}
        keep = set()
        for d_ in o.deps:
            if d_.is_dma:
                keep.add(d_)
            elif d_.eng not in best or best[d_.eng].idx < d_.idx:
                best[d_.eng] = d_
        o.deps = keep | set(best.values())
        self.ops.append(o)
        return o

    def barrier(self):
        self.frontier = list(self.arena_ops.values()) + list(self.arena_dmas)
        self.arena_dmas = []

    def emit(self, nc, stack):
        ops = self.ops
        for o in ops:
            for d in o.deps:
                if d.is_dma:
                    continue
                if d.eng == o.eng and o.eng == "pe" and not o.is_dma:
                    continue
                d.need_inc = True
        sems = {e: stack.enter_context(nc.semaphore("s_" + e)) for e in ("pe", "act", "dve", "pool")}
        ndsem = 24
        dsems = [stack.enter_context(nc.semaphore("d%d" % i)) for i in range(ndsem)]
        cnt = {e: 0 for e in sems}
        per_eng = {e: [] for e in ("pe", "act", "dve", "pool", "sp")}
        rr = {e: 0 for e in self.DMA_SLOTS}
        slot_tot = [0] * ndsem
        slot_prev = [None] * ndsem
        dma_prev = {}
        for o in ops:
            if o.is_dma:
                base, n = self.DMA_SLOTS[o.eng]
                o.dma_slot = base + rr[o.eng] % n
                rr[o.eng] += 1
                slot_tot[o.dma_slot] += 16 * o.nq
                o.dma_val = slot_tot[o.dma_slot]
                dma_prev[o] = slot_prev[o.dma_slot]
                slot_prev[o.dma_slot] = o
            elif o.need_inc:
                cnt[o.eng] += 1
                o.inc_val = cnt[o.eng]
            per_eng[o.eng].append(o)
        self.stats = dict(n_ops=len(ops), cnt=dict(cnt), per_eng={e: len(v) for e, v in per_eng.items()})

        def run_engine(ename, eng):
            known = {}

            def wait(sem, key, val):
                if known.get(key, 0) >= val:
                    return
                eng.wait_ge(sem, val)
                known[key] = val

            for o in per_eng[ename]:
                if o.is_dma and dma_prev[o] is not None:
                    p = dma_prev[o]
                    wait(dsems[p.dma_slot], ("d", p.dma_slot), p.dma_val)
                for d in sorted(o.deps, key=lambda z: z.idx):
                    if d.is_dma:
                        wait(dsems[d.dma_slot], ("d", d.dma_slot), d.dma_val)
                    else:
                        if d.eng == ename and ename == "pe" and not o.is_dma:
                            continue
                        wait(sems[d.eng], d.eng, d.inc_val)
                ins = o.fn(eng)
                if o.is_dma:
                    ilist = ins if isinstance(ins, (list, tuple)) else [ins]
                    assert len(ilist) == o.nq
                    for i_ in ilist:
                        i_.then_inc(dsems[o.dma_slot], 16)
                elif o.need_inc:
                    ins.then_inc(sems[ename], 1)

        block = stack.enter_context(nc.Block())
        last_dma = [p for p in slot_prev if p is not None]
        last_inc = dict(cnt)

        @block.tensor
        def _(e):
            run_engine("pe", e)

        @block.scalar
        def _(e):
            run_engine("act", e)

        @block.vector
        def _(e):
            run_engine("dve", e)

        @block.gpsimd
        def _(e):
            run_engine("pool", e)

        @block.sync
        def _(e):
            run_engine("sp", e)
            for p in last_dma:
                e.wait_ge(dsems[p.dma_slot], p.dma_val)
            for en, v in last_inc.items():
                if v > 0:
                    e.wait_ge(sems[en], v)


def build(nsub=12):
    nc = bass.Bass("TRN2", target_bir_lowering=False)
    DEPTH = max(1, (nsub + 2) // 3)
    di = lambda name, shape, dt=F32: nc.dram_tensor(name, list(shape), dt, kind="ExternalInput").ap()
    x_d = di("x", [S, D])
    cT_d = di("cT", [P, 8])
    pos_d = di("pos", [1, S], I32)
    wada_d = di("w_ada", [DEPTH, D, 9 * D])
    badaT_d = di("b_adaT", [DEPTH, P, 72])
    bada_d = di("b_ada", [DEPTH, 9 * D])
    ngT_d = di("norm_gT", [DEPTH, P, 48])
    ng_d = di("norm_g", [DEPTH, 6, D])
    win_d = [di("w_in1", [DEPTH, D, 2 * DFF]), di("w_in2", [DEPTH, D, 2 * DFF])]
    wout_d = [di("w_out1", [DEPTH, DFF, D]), di("w_out2", [DEPTH, DFF, D])]
    wmix_d = di("w_mix", [DEPTH, D, MIXC])
    wmo_d = di("w_mo", [DEPTH, D, D])
    wblk_d = di("wblk", [DEPTH, P, 16, P])
    lrup_d = di("lrup", [DEPTH, P, 44])
    lamqk_d = di("lamqk", [DEPTH, 256])
    subln_d = di("subln", [DEPTH, P])
    lcst_d = di("lcst", [DEPTH, P, 2])
    ident_d = di("ident", [P, P])
    cst_d = di("cst", [P, 4])
    y_d = nc.dram_tensor("y", [S, D], F32, kind="ExternalOutput").ap()

    st = ExitStack()
    with st:
        sb = lambda name, shape, dt=F32: st.enter_context(nc.sbuf_tensor("sb_" + name, list(shape), dt))
        X = sb("X", [P, NT, D])
        WB = sb("WB", [P, NSLOT, 2048], BF16)
        AR = sb("AR", [P, ARW])
        ident = sb("identb", [P, P], BF16)
        cst = sb("cst", [P, 4])
        c_sb = sb("c_sb", [P, 8])
        cond_f = sb("cond_f", [P, 8])
        condT = sb("condT", [P, 8], BF16)
        cond_rep = sb("cond_rep", [P, 8, P], BF16)
        ggb = sb("ggb", [P, D])
        adaT = sb("adaT", [P, 16])
        bsh = sb("bsh", [P, 16])
        gpre = sb("gpre", [P, 8])
        gsT = sb("gsT", [P, 8])
        ss = sb("ss", [P, 16])
        sd = sb("sd", [P, 16])
        rstd = sb("rstd", [P, 16])
        ss2 = sb("ss2", [P, 16])
        sd2 = sb("sd2", [P, 16])
        rstd2 = sb("rstd2", [P, 16])
        lrup = sb("lrup", [P, 44])
        lsm = sb("lsm", [P, 96])
        lamb = sb("lamb", [P, 256])
        lprod = sb("lprod", [P, 128])
        lsc = sb("lsc", [P, 8])
        subg = sb("subg", [P, P])
        asm = sb("asm", [P, 64])
        carry = sb("carry", [P, 2])
        lcst = sb("lcst", [P, 2])
        Blcst = Buf("lcst")
        if os.environ.get("MK_PSBF"):
            PSD = [st.enter_context(nc.psum_tensor("psd%d" % i, [P, 1024], F32)) for i in range(2)] + \
                  [st.enter_context(nc.psum_tensor("psd%d" % i, [P, 2048], BF16)) for i in range(2, 4)]
        else:
            PSD = [st.enter_context(nc.psum_tensor("psd%d" % i, [P, 1024], F32)) for i in range(4)]

        Sx = Sched()
        BX = [Buf("X%d" % t) for t in range(NT)]
        BPS = [[Buf("ps%d_%d" % (i, h)) for h in range(2)] for i in range(4)]
        Bident, Bcst, Bcond = Buf("ident"), Buf("cst"), Buf("cond")
        Bggb, Bada, Bgs = Buf("ggb"), Buf("adaT"), Buf("gsT")
        Bbsh, Bgpre = Buf("bsh"), Buf("gpre")
        Bss, Bsd, Brstd = Buf("ss"), Buf("sd"), Buf("rstd")
        Bss2 = [Buf("ss2_%d" % t) for t in range(NT)]
        Blrup, Blsm, Blamb, Blsc, Bsubg = Buf("lrup"), Buf("lsm"), Buf("lamb"), Buf("lsc"), Buf("subg")
        Bcarry = Buf("carry")
        BWB = [Buf("wb%d" % i) for i in range(NSLOT)]
        wcount = [0]

        def wpiece(src, a, b):
            s = wcount[0] % NSLOT
            wcount[0] += 1
            view = WB[:, s, :].rearrange("p (a b) -> p a b", a=a)
            Sx.op("pool", lambda e: e.dma_start(out=view, in_=src), writes=[BWB[s]], dma=True)
            return view, BWB[s]

        def av(off_w, nwords, dt=F32):
            assert off_w + nwords <= ARW, (off_w, nwords)
            v = AR[:, off_w:off_w + nwords]
            return v.bitcast(BF16) if dt == BF16 else v

        def ab(name):
            return Buf(name, arena=True)

        def mm(out, lhsT, rhs, start, stop, reads, writes):
            Sx.op("pe", lambda e: e.matmul(out, lhsT=lhsT, rhs=rhs, start=start, stop=stop), reads, writes)

        def tp(out, in_, reads, writes):
            Sx.op("pe", lambda e: e.transpose(out, in_, ident[:]), list(reads) + [Bident], writes)

        def act(out, in_, func, reads, writes, bias=None, scale=None, accum=None):
            kw = {}
            if bias is not None:
                kw["bias"] = bias
            if scale is not None:
                kw["scale"] = scale
            if accum is not None:
                kw["accum_out"] = accum
            Sx.op("act", lambda e: e.activation(out=out, in_=in_, func=func, **kw), reads, writes)

        def tt(out, in0, in1, op, reads, writes):
            Sx.op("dve", lambda e: e.tensor_tensor(out=out, in0=in0, in1=in1, op=op), reads, writes)

        def ts(out, in0, s1, op0, reads, writes, s2=None, op1=None):
            if op1 is None:
                Sx.op("dve", lambda e: e.tensor_scalar(out=out, in0=in0, scalar1=s1, scalar2=None, op0=op0),
                      reads, writes)
            else:
                Sx.op("dve", lambda e: e.tensor_scalar(out=out, in0=in0, scalar1=s1, scalar2=s2, op0=op0, op1=op1),
                      reads, writes)

        def stt(out, in0, scalar, in1, op0, op1, reads, writes):
            Sx.op("dve", lambda e: e.scalar_tensor_tensor(out=out, in0=in0, scalar=scalar, in1=in1, op0=op0, op1=op1),
                  reads, writes)

        def recip(out, in_, reads, writes):
            Sx.op("dve", lambda e: e.reciprocal(out=out, in_=in_), reads, writes)

        def vcopy(out, in_, reads, writes):
            Sx.op("dve", lambda e: e.tensor_copy(out=out, in_=in_), reads, writes)

        def dma_sp(out, in_, reads, writes):
            Sx.op("sp", lambda e: e.dma_start(out=out, in_=in_), reads, writes, dma=True)

        for t in range(NT):
            dma_sp(X[:, t, :], x_d[t * P:(t + 1) * P, :], [], [BX[t]])
        Sx.op("pool", lambda e: e.dma_start(out=ident[:], in_=ident_d), writes=[Bident], dma=True)
        dma_sp(cst[:], cst_d, [], [Bcst])
        dma_sp(c_sb[:], cT_d, [], [Bcond])
        act(cond_f[:], c_sb[:], AF.Silu, [Bcond], [Bcond])
        vcopy(condT[:], cond_f[:], [Bcond], [Bcond])
        vcopy(cond_rep[:], cond_f[:].unsqueeze(2).to_broadcast([P, 8, P]), [Bcond], [Bcond])

        def ada_phase(l, j, res_w, off_tmp):
            bgb = av(off_tmp, 1024)
            gpb = av(off_tmp + 1024, 1024)
            Bbg, Bgp = ab("bgb"), ab("gpb")
            dma_sp(bgb, bada_d[l, j * 3072 + 2048: j * 3072 + 3072].partition_broadcast(P), [], [Bbg])
            dma_sp(gpb, ng_d[l, 2 * j + 1, :].partition_broadcast(P), [], [Bgp])
            dma_sp(bsh[:], badaT_d[l, :, j * 24: j * 24 + 16], [], [Bbsh])
            dma_sp(gpre[:], ngT_d[l, :, (2 * j) * 8:(2 * j) * 8 + 8], [], [Bgpre])
            base = j * 3072
            for pi in range(8):
                wv, bw = wpiece(wada_d[l][:, base + pi * 256: base + (pi + 1) * 256]
                                .rearrange("(k p) n -> p k n", p=P), 8, 256)
                for sub in range(2):
                    idx = pi * 2 + sub
                    for kk in range(8):
                        mm(PSD[0][:, idx:idx + 1], wv[:, kk, sub * 128:(sub + 1) * 128], condT[:, kk:kk + 1],
                           kk == 0, kk == 7, [bw, Bcond], [BPS[0][0]])
            for pi in range(4):
                wv, bw = wpiece(wada_d[l][:, base + 2048 + pi * 256: base + 2048 + (pi + 1) * 256]
                                .rearrange("(k p) n -> p k n", p=P), 8, 256)
                for kk in range(8):
                    mm(PSD[1][:, pi * 256:(pi + 1) * 256], cond_rep[:, kk, :], wv[:, kk, :],
                       kk == 0, kk == 7, [bw, Bcond], [BPS[1][pi // 2]])
            tt(adaT[:], PSD[0][:, 0:16], bsh[:], ALU.add, [BPS[0][0], Bbsh], [Bada])
            stt(gsT[:], adaT[:, 8:16], 1.0, gpre[:], ALU.add, ALU.mult, [Bada, Bgpre], [Bgs])
            tt(bgb, PSD[1][:, :], bgb, ALU.add, [BPS[1][0], BPS[1][1], Bbg], [Bbg])
            stt(ggb[:], bgb, float(res_w), gpb, ALU.mult, ALU.mult, [Bbg, Bgp], [Bggb])

        def prenorm(tiles, hT, BhT, off_tmp):
            hb = [av(off_tmp + i * 512, 512, BF16) for i in range(2)]
            sqj = av(off_tmp + 1024, 512, BF16)
            Bhb = [ab("hb0"), ab("hb1")]
            Bsq = ab("sqj")
            t0, t1 = tiles[0], tiles[-1] + 1
            for t in tiles:
                act(sqj, X[:, t, :], AF.Square, [BX[t]], [Bsq, Bss], accum=ss[:, t:t + 1])
            act(sd[:, t0:t1], ss[:, t0:t1], AF.Sqrt, [Bss, Bcst], [Bsd], bias=cst[:, 2:3], scale=1.0 / D)
            recip(rstd[:, t0:t1], sd[:, t0:t1], [Bsd], [Brstd])
            EV = os.environ.get("MK_EVAC", "dve")
            for ti, t in enumerate(tiles):
                pi, sub = ti // 2, ti % 2
                par = ti % 2
                pd = PSD[2 + ti % 2]
                bp = BPS[2 + ti % 2]
                ts(hb[par], X[:, t, :], rstd[:, t:t + 1], ALU.mult, [BX[t], Brstd], [Bhb[par]])
                for k in range(8):
                    mm(pd[:, k * 128:(k + 1) * 128], hb[par][:, k * 128:(k + 1) * 128], ident[:], True, True,
                       [Bhb[par], Bident], [bp[k // 4]])
                for k in range(8):
                    o = hT[:, k, ti * 128:(ti + 1) * 128]
                    src = pd[:, k * 128:(k + 1) * 128]
                    if EV == "copy":
                        vcopy(o, src, [bp[k // 4]], [BhT[pi][k]])
                    elif (k % 2 == 0 and EV == "orig") or EV == "act":
                        act(o, src, AF.Identity, [bp[k // 4], Bgs, Bada], [BhT[pi][k]],
                            bias=adaT[:, k:k + 1], scale=gsT[:, k:k + 1])
                    else:
                        ts(o, src, gsT[:, k:k + 1], ALU.mult, [bp[k // 4], Bgs, Bada], [BhT[pi][k]],
                           s2=adaT[:, k:k + 1], op1=ALU.add)

        def postnorm(i, T, tmpf, Btmp, sqj2, Bsq2):
            pd = PSD[i]
            bp = BPS[i]
            act(sqj2, pd[:, :], AF.Square, [bp[0], bp[1]], [Bsq2, Bss2[T]], accum=ss2[:, T:T + 1])
            act(sd2[:, T:T + 1], ss2[:, T:T + 1], AF.Sqrt, [Bss2[T], Bcst], [Bss2[T]], bias=cst[:, 2:3], scale=1.0 / D)
            recip(rstd2[:, T:T + 1], sd2[:, T:T + 1], [Bss2[T]], [Bss2[T]])
            stt(tmpf, pd[:, :], rstd2[:, T:T + 1], ggb[:], ALU.mult, ALU.mult,
                [bp[0], bp[1], Bss2[T], Bggb], [Btmp])
            tt(X[:, T, :], X[:, T, :], tmpf, ALU.add, [BX[T], Btmp], [BX[T]])

        def ffn(l, which, j):
            Sx.barrier()
            o_hT, o_act = 0, 4096
            o_tmp = o_act + 11264
            o_pre = o_tmp
            o_pre2 = o_tmp + 2048
            o_sg = o_pre2 + 1536
            o_tmpf = o_sg + 1024
            o_sq2 = o_tmpf + 2048
            assert o_sq2 + 512 <= ARW
            hT = av(o_hT, 4096, BF16).rearrange("p (k t) -> p k t", k=8)
            actT = av(o_act, 11264, BF16).rearrange("p (c t) -> p c t", c=FC)
            sg = [av(o_sg + i * 512, 512) for i in range(2)]
            tmpf = [av(o_tmpf + i * 1024, 1024) for i in range(2)]
            sqj2 = av(o_sq2, 512, BF16)
            Bsg = [ab("sg0"), ab("sg1")]
            Btmpf = [ab("tmpf0"), ab("tmpf1")]
            Bsq2 = ab("sqj2")
            STOP = int(os.environ.get("MK_STOP", "99"))
            if STOP >= 1:
                ada_phase(l, j, 0.5, o_pre)
            w_in, w_out = win_d[which][l], wout_d[which][l]
            BhT = [[ab("hT%d_%d" % (pi, k)) for k in range(8)] for pi in range(4)]
            Bact = [[ab("act%d_%d" % (c, tc)) for tc in range(2)] for c in range(FC)]
            for blk in range(2):
                tiles = list(range(blk * 8, blk * 8 + 8))
                if STOP >= 2:
                    prenorm(tiles, hT, BhT, o_pre2)
                if STOP < 3:
                    return
                for c in range(FC):
                    wv, bw = wpiece(w_in[:, c * 256:(c + 1) * 256].rearrange("(k p) n -> p k n", p=P), 8, 256)
                    for tc in range(2):
                        i = (c * 2 + tc) % 4
                        pd, bp = PSD[i], BPS[i]
                        for half in range(2):
                            for k in range(8):
                                mm(pd[:, half * 512:(half + 1) * 512], wv[:, k, half * 128:(half + 1) * 128],
                                   hT[:, k, tc * 512:(tc + 1) * 512], k == 0, k == 7,
                                   [bw, BhT[tc * 2][k], BhT[tc * 2 + 1][k]], [bp[half]])
                        par = (c * 2 + tc) % 2
                        act(sg[par], pd[:, 0:512], AF.Silu, [bp[0]], [Bsg[par]])
                        tt(actT[:, c, tc * 512:(tc + 1) * 512], sg[par], pd[:, 512:1024], ALU.mult,
                           [Bsg[par], bp[1]], [Bact[c][tc]])
                if STOP < 4:
                    return
                for tg in range(2):
                    for c2 in range(FC // 2):
                        wv, bw = wpiece(w_out[c2 * 256:(c2 + 1) * 256, :].rearrange("(c p) n -> p c n", p=P), 2, 1024)
                        for cc in range(2):
                            c = c2 * 2 + cc
                            for ti in range(4):
                                tl = tg * 4 + ti
                                for half in range(2):
                                    mm(PSD[ti][:, half * 512:(half + 1) * 512], actT[:, c, tl * 128:(tl + 1) * 128],
                                       wv[:, cc, half * 512:(half + 1) * 512], c == 0, c == FC - 1,
                                       [bw, Bact[c][tl // 4]], [BPS[ti][half]])
                    if STOP < 5:
                        continue
                    for ti in range(4):
                        T = blk * 8 + tg * 4 + ti
                        postnorm(ti, T, tmpf[ti % 2], Btmpf[ti % 2], sqj2, Bsq2)

        def mixer(l):
            lambda_init = 0.8 - 0.6 * math.exp(-0.3 * l)
            Sx.barrier()
            o_hT = 0
            o_rec = 8192
            o_q = 12288
            o_k = 16384
            o_v = 20480
            assert o_v + 4128 <= ARW
            hT = av(o_hT, 8192, BF16).rearrange("p (k t) -> p k t", k=8)
            recT = av(o_rec, 4096, BF16).rearrange("p (k t) -> p k t", k=4)
            BhT = [[ab("hT%d_%d" % (pi, k)) for k in range(8)] for pi in range(8)]
            Brec = [[ab("rec%d_%d" % (cc, hh)) for hh in range(2)] for cc in range(4)]

            ada_phase(l, 1, 1.0, o_q)
            prenorm(list(range(NT)), hT, BhT, o_q + 2048)
            dma_sp(lrup[:], lrup_d[l], [], [Blrup])
            dma_sp(lamb[:], lamqk_d[l, :].partition_broadcast(P), [], [Blamb])
            dma_sp(subg[:], subln_d[l, :].partition_broadcast(P), [], [Bsubg])
            tt(lprod[:], lamb[:, 0:128], lamb[:, 128:256], ALU.mult, [Blamb], [Blamb])
            Sx.op("dve", lambda e: e.tensor_reduce(out=lsc[:, 0:2], in_=lprod[:].rearrange("p (a b) -> p a b", a=2),
                                                   axis=AX.X, op=ALU.add), [Blamb], [Blsc])
            act(lsc[:, 2:4], lsc[:, 0:2], AF.Exp, [Blsc], [Blsc])
            tt(lsc[:, 4:5], lsc[:, 2:3], lsc[:, 3:4], ALU.subtract, [Blsc], [Blsc])
            dma_sp(lcst[:], lcst_d[l], [], [Blcst])
            tt(lsc[:, 6:7], lsc[:, 4:5], lcst[:, 0:1], ALU.add, [Blsc, Blcst], [Blsc])
            ts(lsc[:, 5:6], lsc[:, 6:7], -1.0, ALU.mult, [Blsc], [Blsc])
            ts(subg[:], subg[:], lcst[:, 1:2], ALU.mult, [Bsubg, Blcst], [Bsubg])
            lam_ap = lrup[:, 36:44]
            ts(lsm[:, 0:8], lam_ap, -1.0, ALU.mult, [Blrup], [Blsm])
            ts(lsm[:, 8:16], lsm[:, 0:8], -1.0, ALU.mult, [Blsm], [Blsm])
            tt(lsm[:, 8:16], lsm[:, 8:16], lsm[:, 0:8], ALU.max, [Blsm], [Blsm])
            act(lsm[:, 16:24], lsm[:, 8:16], AF.Exp, [Blsm], [Blsm], scale=-1.0)
            act(lsm[:, 24:32], lsm[:, 16:24], AF.Ln, [Blsm, Bcst], [Blsm], bias=cst[:, 3:4])
            ts(lsm[:, 32:40], lsm[:, 16:24], -1.0 / 3.0, ALU.mult, [Blsm], [Blsm], s2=0.5, op1=ALU.add)
            tt(lsm[:, 32:40], lsm[:, 32:40], lsm[:, 16:24], ALU.mult, [Blsm], [Blsm])
            ts(lsm[:, 32:40], lsm[:, 32:40], -1.0, ALU.mult, [Blsm], [Blsm], s2=1.0, op1=ALU.add)
            tt(lsm[:, 32:40], lsm[:, 32:40], lsm[:, 16:24], ALU.mult, [Blsm], [Blsm])
            ts(lsm[:, 40:48], lsm[:, 16:24], 0.03, ALU.is_lt, [Blsm], [Blsm])
            tt(lsm[:, 32:40], lsm[:, 32:40], lsm[:, 24:32], ALU.subtract, [Blsm], [Blsm])
            tt(lsm[:, 32:40], lsm[:, 32:40], lsm[:, 40:48], ALU.mult, [Blsm], [Blsm])
            tt(lsm[:, 48:56], lsm[:, 32:40], lsm[:, 24:32], ALU.add, [Blsm], [Blsm])
            ts(lsm[:, 72:80], lsm[:, 0:8], 0.0, ALU.max, [Blsm], [Blsm])
            tt(lsm[:, 48:56], lsm[:, 48:56], lsm[:, 72:80], ALU.add, [Blsm], [Blsm])
            ts(lsm[:, 56:64], lsm[:, 48:56], -8.0, ALU.mult, [Blsm], [Blsm])
            ts(lsm[:, 64:72], lsm[:, 48:56], -16.0, ALU.mult, [Blsm], [Blsm])

            o_l = 12288
            o_xr = o_l
            o_xc = o_xr + 2052
            o_xcb = o_xc + 2048
            o_hf = o_xcb + 1024
            o_T1 = o_hf + 2048
            o_gw = o_T1 + 3072
            assert o_gw + 1024 <= ARW
            Sx.barrier()
            gw = av(o_gw, 1024, BF16).rearrange("p (a b) -> p a b", a=16)
            Bgw = ab("gw")
            Sx.op("pool", lambda e: e.dma_start(out=gw, in_=wblk_d[l]), writes=[Bgw], dma=True)
            xr_pad = av(o_xr, 2052)
            xc = av(o_xc, 2048)
            xcb = av(o_xcb, 1024, BF16)
            hf = av(o_hf, 2048)
            T1, T2, T3 = av(o_T1, 1024), av(o_T1 + 1024, 1024), av(o_T1 + 2048, 1024)
            Bxr, Bxc, Bxcb, Bhf, BT1, BT2, BT3 = ab("xr"), ab("xc"), ab("xcb"), ab("hf"), ab("T1"), ab("T2"), ab("T3")
            Sx.op("dve", lambda e: e.memset(xr_pad[:, 0:2], 0.0), [], [Bxr])
            Sx.op("dve", lambda e: e.memset(xr_pad[:, 2050:2052], 0.0), [], [Bxr])
            for cc in range(4):
                wv, bw = wpiece(wmix_d[l][:, 2560 + cc * 256: 2560 + (cc + 1) * 256]
                                .rearrange("(k p) n -> p k n", p=P), 8, 256)

                def proj(colo):
                    for tc in range(4):
                        pd, bp = PSD[tc // 2], BPS[tc // 2][tc % 2]
                        for k in range(8):
                            mm(pd[:, (tc % 2) * 512:(tc % 2 + 1) * 512], wv[:, k, colo:colo + 128],
                               hT[:, k, tc * 512:(tc + 1) * 512], k == 0, k == 7,
                               [bw, BhT[tc * 2][k], BhT[tc * 2 + 1][k]], [bp])
                proj(0)
                for hh in range(2):
                    act(xr_pad[:, 2 + hh * 1024: 2 + (hh + 1) * 1024], PSD[hh][:, :], AF.Copy,
                        [BPS[hh][0], BPS[hh][1]], [Bxr])
                ts(xc, xr_pad[:, 0:2048], lrup[:, cc * 4:cc * 4 + 1], ALU.mult, [Bxr, Blrup], [Bxc],
                   s2=lrup[:, 16 + cc:17 + cc], op1=ALU.add)
                for jj in range(1, 4):
                    stt(xc, xr_pad[:, jj:jj + 2048], lrup[:, cc * 4 + jj:cc * 4 + jj + 1], xc, ALU.mult, ALU.add,
                        [Bxr, Blrup, Bxc], [Bxc])
                act(xcb, xc, AF.Copy, [Bxc], [Bxcb])
                for dr in range(2):
                    for hp in ((0, 1) if dr == 0 else (1, 0)):
                        for g in range(2):
                            for tcc in range(2):
                                tc = hp * 2 + tcc
                                mm(PSD[2 + g][:, tcc * 512:(tcc + 1) * 512], gw[:, (dr * 2 + g) * 4 + cc, :],
                                   xcb[:, tc * 512:(tc + 1) * 512], True, True, [Bgw, Bxcb], [BPS[2 + g][tcc]])
                        bg0 = lrup[:, 20 + (dr * 2 + 0) * 4 + cc: 21 + (dr * 2 + 0) * 4 + cc]
                        bg1 = lrup[:, 20 + (dr * 2 + 1) * 4 + cc: 21 + (dr * 2 + 1) * 4 + cc]
                        s8 = lsm[:, 56 + dr * 4 + cc: 57 + dr * 4 + cc]
                        s16 = lsm[:, 64 + dr * 4 + cc: 65 + dr * 4 + cc]
                        act(T1, PSD[2][:, :], AF.Sigmoid, [BPS[2][0], BPS[2][1], Blrup], [BT1], bias=bg0)
                        act(T3, PSD[3][:, :], AF.Sigmoid, [BPS[3][0], BPS[3][1], Blrup], [BT3], bias=bg1)
                        act(T2, T1, AF.Exp, [BT1, Blsm], [BT2], scale=s16)
                        act(T1, T1, AF.Exp, [BT1, Blsm], [BT1], scale=s8)
                        act(T2, T2, AF.Sqrt, [BT2, Bcst], [BT2], bias=cst[:, 3:4], scale=-1.0)
                        tt(T3, T3, xc[:, hp * 1024:(hp + 1) * 1024], ALU.mult, [BT3, Bxc], [BT3])
                        tt(T3, T3, T2, ALU.mult, [BT3, BT2], [BT3])
                        if dr == 0:
                            init = 0.0 if hp == 0 else hf[:, 1023:1024]
                            Sx.op("dve", lambda e, init=init, hp=hp: e.tensor_tensor_scan(
                                out=hf[:, hp * 1024:(hp + 1) * 1024], data0=T1, data1=T3, initial=init,
                                op0=ALU.mult, op1=ALU.add), [BT1, BT3, Bhf], [Bhf])
                        else:
                            init = 0.0 if hp == 1 else carry[:, 0:1]
                            Sx.op("dve", lambda e, init=init: e.tensor_tensor_scan(
                                out=T3[:, ::-1], data0=T1[:, ::-1], data1=T3[:, ::-1], initial=init,
                                op0=ALU.mult, op1=ALU.add), [BT1, BT3, Bcarry], [BT3])
                            if hp == 1:
                                vcopy(carry[:, 0:1], T3[:, 0:1], [BT3], [Bcarry])
                            tt(hf[:, hp * 1024:(hp + 1) * 1024], hf[:, hp * 1024:(hp + 1) * 1024], T3, ALU.add,
                               [Bhf, BT3], [Bhf])
                proj(128)
                for hh in range(2):
                    pd, bp = PSD[hh], BPS[hh]
                    act(T1, pd[:, :], AF.Square, [bp[0], bp[1]], [BT1])
                    ts(T2, T1, 0.044715, ALU.mult, [BT1], [BT2], s2=1.0, op1=ALU.add)
                    tt(T2, T2, pd[:, :], ALU.mult, [BT2, bp[0], bp[1]], [BT2])
                    act(T2, T2, AF.Sigmoid, [BT2], [BT2], scale=1.5957691216057308)
                    tt(T2, T2, pd[:, :], ALU.mult, [BT2, bp[0], bp[1]], [BT2])
                    tt(recT[:, cc, hh * 1024:(hh + 1) * 1024], T2, hf[:, hh * 1024:(hh + 1) * 1024], ALU.mult,
                       [BT2, Bhf], [Brec[cc][hh]])

            Sx.barrier()
            Ctab = av(o_v, 2048)
            Stab = av(o_v + 2048, 2048)
            BC, BS = ab("Ctab"), ab("Stab")
            posi = av(o_q, 2048).bitcast(I32)
            ang = av(o_q + 2048, 2048)
            yv = av(o_k, 2048)
            nf = av(o_k + 2048, 2048)
            Bpos, Bang, Byv, Bnf = ab("posi"), ab("ang"), ab("yv"), ab("nf")
            dma_sp(posi, pos_d[0, :].partition_broadcast(P), [], [Bpos])
            vcopy(ang, posi, [Bpos], [Bang])
            ts(ang, ang, cst[:, 0:1], ALU.mult, [Bang, Bcst], [Bang])
            TWO_PI = 2.0 * math.pi
            for which_t, (tab, Bt) in enumerate(((Stab, BS), (Ctab, BC))):
                if which_t == 0:
                    vcopy(yv, ang, [Bang], [Byv])
                else:
                    ts(yv, ang, math.pi / 2.0, ALU.add, [Bang], [Byv])
                ni = nf.bitcast(I32)
                ts(ni, yv, 1.0 / TWO_PI, ALU.mult, [Byv], [Bnf])
                vcopy(tab, ni, [Bnf], [Bt])
                stt(yv, tab, -TWO_PI, yv, ALU.mult, ALU.add, [Bt, Byv], [Byv])
                ts(nf, yv, math.pi, ALU.is_gt, [Byv], [Bnf], s2=-TWO_PI, op1=ALU.mult)
                tt(yv, yv, nf, ALU.add, [Byv, Bnf], [Byv])
                ts(nf, yv, -math.pi, ALU.is_lt, [Byv], [Bnf], s2=TWO_PI, op1=ALU.mult)
                tt(yv, yv, nf, ALU.add, [Byv, Bnf], [Byv])
                if which_t == 0:
                    act(tab, yv, AF.Sin, [Byv, Bcst], [Bt], scale=cst[:, 1:2])
                else:
                    act(tab, yv, AF.Sin, [Byv], [Bt])

            Sx.barrier()
            qT = av(o_q, 4096, BF16).rearrange("p (k t) -> p k t", k=4)
            kT = av(o_k, 4096, BF16).rearrange("p (k t) -> p k t", k=4)
            Bq = [[ab("q%d_%d" % (h, tc)) for tc in range(4)] for h in range(4)]
            Bk = [[ab("k%d_%d" % (h, tc)) for tc in range(4)] for h in range(4)]
            cnt_i = 0
            rope1 = av(o_v + 4096, 256)
            rope2 = av(o_v + 4096 + 256, 256)
            Brope1, Brope2 = ab("rope1"), ab("rope2")
            for h in range(4):
                for wh, (dst, Bd) in enumerate(((qT, Bq), (kT, Bk))):
                    wv, bw = wpiece(wmix_d[l][:, h * 512 + wh * 256: h * 512 + (wh + 1) * 256]
                                    .rearrange("(k p) n -> p k n", p=P), 8, 256)
                    for tc in range(4):
                        i = cnt_i % 4
                        cnt_i += 1
                        pd, bp = PSD[i], BPS[i]
                        for half in range(2):
                            for k in range(8):
                                mm(pd[:, half * 512:(half + 1) * 512], wv[:, k, half * 128:(half + 1) * 128],
                                   hT[:, k, tc * 512:(tc + 1) * 512], k == 0, k == 7,
                                   [bw, BhT[tc * 2][k], BhT[tc * 2 + 1][k]], [bp[half]])
                        for ch in range(2):
                            c0 = tc * 512 + ch * 256
                            tt(rope1, pd[:, ch * 256:(ch + 1) * 256], Ctab[:, c0:c0 + 256], ALU.mult,
                               [bp[0], BC], [Brope1])
                            tt(rope2, pd[:, 512 + ch * 256:512 + (ch + 1) * 256], Stab[:, c0:c0 + 256], ALU.mult,
                               [bp[1], BS], [Brope2])
                            tt(dst[:, h, c0:c0 + 256], rope1, rope2, ALU.add, [Brope1, Brope2], [Bd[h][tc]])

            Sx.barrier()
            vaug = av(o_v, 4128, BF16).rearrange("p (t h e) -> p t h e", t=16, h=4)
            Bv = [[ab("v%d_%d" % (t, pi)) for pi in range(2)] for t in range(NT)]
            Bones = ab("vones")
            Sx.op("dve", lambda e: e.memset(vaug[:, :, :, 128:129], 1.0), [], [Bones])
            for pi in range(2):
                wv, bw = wpiece(wmix_d[l][:, 2048 + pi * 256: 2048 + (pi + 1) * 256]
                                .rearrange("(k p) n -> p k n", p=P), 8, 256)
                for t in range(NT):
                    i = t % 4
                    for k in range(8):
                        mm(PSD[i][:, 0:256], hT[:, k, t * 128:(t + 1) * 128], wv[:, k, :], k == 0, k == 7,
                           [bw, BhT[t // 2][k]], [BPS[i][0]])
                    src = PSD[i][:, 0:256].rearrange("p (h e) -> p h e", h=2)
                    dstv = vaug[:, t, 2 * pi:2 * pi + 2, 0:128]
                    if t % 2 == 0:
                        act(dstv, src, AF.Copy, [BPS[i][0]], [Bv[t][pi]])
                    else:
                        vcopy(dstv, src, [BPS[i][0]], [Bv[t][pi]])

            Sx.barrier()
            attnT = av(o_hT, 4096, BF16).rearrange("p (k t) -> p k t", k=4)
            Battn = [[ab("attn%d_%d" % (h, t)) for t in range(NT)] for h in range(4)]
            o_t = 4096
            pT = [av(o_t + i * 256, 256, BF16) for i in range(3)]
            BpT = [ab("pT%d" % i) for i in range(3)]
            O0 = av(o_t + 768, 512).rearrange("p (q e) -> p q e", q=4)
            BO0 = [ab("O0_%d" % i) for i in range(4)]
            dif = [av(o_t + 1280 + i * 128, 128) for i in range(2)]
            Bdif = [ab("dif0"), ab("dif1")]
            atok = [av(o_t + 1536 + i * 64, 64, BF16) for i in range(2)]
            Batok = [ab("atok0"), ab("atok1")]
            sqa = av(o_t + 1664, 64, BF16)
            Bsqa = ab("sqa")
            Basm = [Buf("asm%d" % i) for i in range(8)]
            PT3 = PSD[3][:, :].bitcast(BF16)
            n_sm = 0
            n_tp = 0
            for h in range(4):
                for qc in range(4):
                    for m in range(2):
                        for kt in range(NT):
                            sc = PSD[0][:, (kt % 2) * 512:(kt % 2 + 1) * 512]
                            bsc = BPS[0][kt % 2]
                            mm(sc, kT[m * 64:(m + 1) * 64, h, kt * 128:(kt + 1) * 128],
                               qT[m * 64:(m + 1) * 64, h, qc * 512:(qc + 1) * 512], True, True,
                               [Bk[h][kt // 4], Bq[h][qc]], [bsc])
                            pi_ = kt % 3
                            act(pT[pi_], sc, AF.Exp, [bsc], [BpT[pi_]], scale=0.125)
                            for qi in range(4):
                                po = PSD[1 + qi // 2][:, (qi % 2) * 512:(qi % 2) * 512 + 129]
                                mm(po, pT[pi_][:, qi * 128:(qi + 1) * 128], vaug[:, kt, h, :], kt == 0, kt == NT - 1,
                                   [BpT[pi_], Bv[kt][h // 2], Bones], [BPS[1 + qi // 2][qi % 2]])
                        for qi in range(4):
                            po = PSD[1 + qi // 2][:, (qi % 2) * 512:(qi % 2) * 512 + 129]
                            bpo = BPS[1 + qi // 2][qi % 2]
                            si = n_sm % 8
                            n_sm += 1
                            sm_ = asm[:, si * 8:(si + 1) * 8]
                            bs_ = Basm[si]
                            recip(sm_[:, 0:1], po[:, 128:129], [bpo], [bs_])
                            if m == 0:
                                ts(O0[:, qi, :], po[:, 0:128], sm_[:, 0:1], ALU.mult, [bpo, bs_], [BO0[qi]])
                            else:
                                par = qi % 2
                                tt(sm_[:, 1:2], sm_[:, 0:1], lsc[:, 5:6], ALU.mult, [bs_, Blsc], [bs_])
                                stt(dif[par], po[:, 0:128], sm_[:, 1:2], O0[:, qi, :], ALU.mult, ALU.add,
                                    [bpo, bs_, BO0[qi]], [Bdif[par]])
                                act(sqa, dif[par], AF.Square, [Bdif[par]], [Bsqa, bs_], accum=sm_[:, 2:3])
                                act(sm_[:, 3:4], sm_[:, 2:3], AF.Sqrt, [bs_, Bcst], [bs_], bias=cst[:, 2:3], scale=1.0 / 128.0)
                                recip(sm_[:, 4:5], sm_[:, 3:4], [bs_], [bs_])
                                stt(atok[par], dif[par], sm_[:, 4:5], subg[:], ALU.mult, ALU.mult,
                                    [Bdif[par], bs_, Bsubg], [Batok[par]])
                                T = qc * 4 + qi
                                slot = n_tp % 4
                                n_tp += 1
                                pt_ = PSD[3][:, slot * 128:(slot + 1) * 128]
                                bpt = Bpt3[slot]
                                mm(pt_, atok[par], ident[:], True, True, [Batok[par], Bident], [bpt])
                                vcopy(attnT[:, h, T * 128:(T + 1) * 128], pt_, [bpt], [Battn[h][T]])

            tmpf = [av(o_q + i * 1024, 1024) for i in range(2)]
            sqj2 = av(o_q + 2048, 512, BF16)
            Sx.barrier()
            Btmpf = [ab("tmpf0"), ab("tmpf1")]
            Bsq2 = ab("sqj2")
            for tg in range(4):
                for pi in range(4):
                    wv, bw = wpiece(wmo_d[l][pi * 256:(pi + 1) * 256, :].rearrange("(c p) n -> p c n", p=P), 2, 1024)
                    for cc in range(2):
                        k = pi * 2 + cc
                        for ti in range(4):
                            T = tg * 4 + ti
                            if k < 4:
                                lhs, rb = attnT[:, k, T * 128:(T + 1) * 128], Battn[k][T]
                            else:
                                lhs, rb = recT[:, k - 4, T * 128:(T + 1) * 128], Brec[k - 4][T // 8]
                            for half in range(2):
                                mm(PSD[ti][:, half * 512:(half + 1) * 512], lhs, wv[:, cc, half * 512:(half + 1) * 512],
                                   k == 0, k == 7, [bw, rb], [BPS[ti][half]])
                for ti in range(4):
                    T = tg * 4 + ti
                    postnorm(ti, T, tmpf[ti % 2], Btmpf[ti % 2], sqj2, Bsq2)

        Bpt3 = [Buf("pt3_%d" % i) for i in range(4)]

        n = 0
        for l in range(4):
            for j in range(3):
                if n >= nsub:
                    break
                if j == 0:
                    ffn(l, 0, 0)
                elif j == 1:
                    mixer(l)
                else:
                    ffn(l, 1, 2)
                n += 1
        for t in range(NT):
            dma_sp(y_d[t * P:(t + 1) * P, :], X[:, t, :], [BX[t]], [])
        Sx.emit(nc, st)
        build.stats = Sx.stats
    return nc


def _prep_shared(inp):
    f = lambda a: np.ascontiguousarray(np.asarray(a, dtype=np.float32))
    sh = {}
    sh["w_ada"] = f(inp["w_ada"])
    b_ada = f(inp["b_ada"])
    sh["b_ada"] = b_ada
    sh["b_adaT"] = f(b_ada.reshape(DEPTH, 3, 3, 8, P).transpose(0, 4, 1, 2, 3).reshape(DEPTH, P, 72))
    ng = f(inp["norm_g"])
    sh["norm_g"] = ng
    sh["norm_gT"] = f(ng.reshape(DEPTH, 6, 8, P).transpose(0, 3, 1, 2).reshape(DEPTH, P, 48))
    for nm, src in (("w_in1", "ffn1_w_in"), ("w_in2", "ffn2_w_in")):
        w = f(inp[src])
        g = w[:, :, :DFF].reshape(DEPTH, D, FC, 1, P)
        u = w[:, :, DFF:].reshape(DEPTH, D, FC, 1, P)
        sh[nm] = f(np.concatenate([g, u], axis=3).reshape(DEPTH, D, 2 * DFF))
    sh["w_out1"] = f(inp["ffn1_w_out"])
    sh["w_out2"] = f(inp["ffn2_w_out"])
    wm = f(inp["w_mix_in"])
    swap = np.arange(128).reshape(2, 2, 32)[:, ::-1, :].reshape(128)
    cols = []
    for h in range(4):
        qc = h * 128 + np.arange(128)
        kc = 512 + h * 128 + np.arange(128)
        cols += [qc, h * 128 + swap, kc, 512 + h * 128 + swap]
    cols.append(1024 + np.arange(512))
    for cc in range(4):
        cols += [1536 + cc * 128 + np.arange(128), 2048 + cc * 128 + np.arange(128)]
    cols = np.concatenate(cols)
    assert cols.shape[0] == MIXC
    sh["w_mix"] = f(wm[:, :, cols])
    sh["w_mo"] = f(inp["w_mix_out"])
    wg = f(inp["lru_w_gate"])
    wblk = np.zeros((DEPTH, P, 16, P), np.float32)
    for dr in range(2):
        for g in range(2):
            for cc in range(4):
                idx = (dr * 2 + g) * 4 + cc
                for bb in range(2):
                    wblk[:, bb * 64:(bb + 1) * 64, idx, bb * 64:(bb + 1) * 64] = wg[:, dr, g, cc * 2 + bb]
    sh["wblk"] = wblk
    cw = f(inp["conv_w"]).reshape(DEPTH, 4, 4, P).transpose(0, 3, 2, 1).reshape(DEPTH, P, 16)
    cb = f(inp["conv_b"]).reshape(DEPTH, 4, P).transpose(0, 2, 1)
    bg = f(inp["lru_b_gate"]).reshape(DEPTH, 2, 2, 4, P).transpose(0, 4, 1, 2, 3).reshape(DEPTH, P, 16)
    lm = f(inp["lru_lambda"]).reshape(DEPTH, 2, 4, P).transpose(0, 3, 1, 2).reshape(DEPTH, P, 8)
    sh["lrup"] = f(np.concatenate([cw, cb, bg, lm], axis=2))
    sh["lamqk"] = f(np.concatenate([f(inp["lambda_q"]).reshape(DEPTH, 128),
                                    f(inp["lambda_k"]).reshape(DEPTH, 128)], axis=1))
    sh["subln"] = f(inp["subln_g"])
    sh["ident"] = np.eye(P, dtype=np.float32)
    cst = np.zeros((P, 4), np.float32)
    pidx = np.arange(P)
    cst[:, 0] = (10000.0 ** (-(np.arange(0, 64, 2, dtype=np.float32)) / 64.0)).astype(np.float32)[pidx % 32]
    cst[:, 1] = np.where((pidx % 64) < 32, -1.0, 1.0)
    cst[:, 2] = EPS
    cst[:, 3] = 1.0
    sh["cst"] = cst
    lc = np.zeros((DEPTH, P, 2), np.float32)
    for l in range(DEPTH):
        li = 0.8 - 0.6 * math.exp(-0.3 * l)
        lc[l, :, 0] = li
        lc[l, :, 1] = 1.0 - li
    sh["lcst"] = lc
    return sh


def kernel(**inputs):
    nsub = int(os.environ.get("MK_NSUB", "12"))
    sh = _prep_shared(inputs)
    x = np.asarray(inputs["x"], dtype=np.float32)
    c = np.asarray(inputs["c"], dtype=np.float32)
    pos = np.asarray(inputs["positions"], dtype=np.int32)
    B = x.shape[0]
    in_maps = []
    for b in range(B):
        m = dict(sh)
        m["x"] = np.ascontiguousarray(x[b])
        m["cT"] = np.ascontiguousarray(c[b].reshape(8, P).T)
        m["pos"] = np.ascontiguousarray(pos[b].reshape(1, S))
        in_maps.append(m)
    ncore = int(os.environ.get("MK_CORES", str(B)))
    in_maps = in_maps[:ncore]
    mode = os.environ.get("MK_MODE", "layers")
    shared_keys = [k_ for k_ in sh.keys() if k_ not in ("ident", "cst")]
    if mode == "layers" and nsub == 12:
        nc = build(3)
        cur = [m["x"] for m in in_maps]
        for L in range(DEPTH):
            maps = []
            for i, m in enumerate(in_maps):
                mm_ = dict(m)
                for k_ in shared_keys:
                    mm_[k_] = np.ascontiguousarray(sh[k_][L:L + 1])
                mm_["x"] = cur[i]
                maps.append(mm_)
            res = run_bass_kernel_spmd(nc, maps, core_ids=list(range(len(maps))))
            cur = [np.ascontiguousarray(r["y"]) for r in res.results]
        return np.stack(cur, axis=0).astype(np.float32)
    nl = max(1, (nsub + 2) // 3)
    if nl < DEPTH:
        for m in in_maps:
            for k_ in shared_keys:
                m[k_] = np.ascontiguousarray(m[k_][:nl])
    nc = build(nsub)
    res = run_bass_kernel_spmd(nc, in_maps, core_ids=list(range(len(in_maps))))
    return np.stack([r["y"] for r in res.results], axis=0).astype(np.float32)
```
